# Optimizing a Trainium2 kernel written in Bass

```python
import math
import jax, jax.numpy as jnp
from jax import lax
import numpy as np

D_MODEL = 2048
BATCH = 4
SEQ = 8192
DEPTH = 4

GDN_DK = 128
GDN_DV = 128
GDN_W = 3 * D_MODEL // 8
GDN_HEADS = GDN_W // GDN_DV
GDN_QK = GDN_HEADS * GDN_DK
ATT_HD = 64
ATT_W = 3 * D_MODEL // 8
ATT_HEADS = ATT_W // ATT_HD
CONV_CH = D_MODEL - GDN_W - ATT_W
MIX_W = CONV_CH + GDN_W + ATT_W

CONV_WIDTH = 31
SHORT_CONV = 4
GDN_CHUNK = 64

ROPE_THETA = 500000.0
ROPE_DIM = ATT_HD // 4
DIL_PATTERNS = ((128, 1), (512, 4), (2048, 16))
ATT_BLOCK = 128
NEG_INF = -1e30

IN_SPLITS = (
    2 * CONV_CH, CONV_CH,
    GDN_QK, GDN_QK, GDN_W, GDN_W, GDN_HEADS, GDN_HEADS,
    ATT_W, ATT_W, ATT_W, ATT_W,
)
IN_W = sum(IN_SPLITS)

kernel_name = "hymba_style_conformer_gdn_dilated_hybrid"


def rms_norm(x, w, eps=1e-6):
    xf = x.astype(jnp.float32)
    y = xf * lax.rsqrt(jnp.mean(xf * xf, axis=-1, keepdims=True) + eps)
    return (y * w.astype(jnp.float32)).astype(x.dtype)


def layer_norm(x, w, b, eps=1e-5):
    xf = x.astype(jnp.float32)
    mu = jnp.mean(xf, axis=-1, keepdims=True)
    var = jnp.mean(jnp.square(xf - mu), axis=-1, keepdims=True)
    y = (xf - mu) * lax.rsqrt(var + eps) * w.astype(jnp.float32) + b.astype(jnp.float32)
    return y.astype(x.dtype)


def l2_normalize(x, eps=1e-6):
    xf = x.astype(jnp.float32)
    return xf * lax.rsqrt(jnp.sum(xf * xf, axis=-1, keepdims=True) + eps)


def causal_dwconv(x, w):
    K, C = w.shape
    return lax.conv_general_dilated(
        x, w[:, None, :].astype(x.dtype), window_strides=(1,), padding=[(K - 1, 0)],
        dimension_numbers=("NWC", "WIO", "NWC"), feature_group_count=C)


def rope_tables(S):
    half = ROPE_DIM // 2
    inv = ROPE_THETA ** (-jnp.arange(half, dtype=jnp.float32) / half)
    ang = jnp.arange(S, dtype=jnp.float32)[:, None] * inv[None, :]
    return jnp.cos(ang), jnp.sin(ang)


def apply_partial_rope(x, cos, sin):
    half = ROPE_DIM // 2
    c, s = cos[None, :, None, :], sin[None, :, None, :]
    x1, x2, rest = x[..., :half], x[..., half:ROPE_DIM], x[..., ROPE_DIM:]
    return jnp.concatenate([x1 * c - x2 * s, x2 * c + x1 * s, rest], axis=-1)


def conformer_conv(u, dw_w, dw_b, ln_w, ln_b, pw_w):
    a, b = jnp.split(u, 2, axis=-1)
    h = a * jax.nn.sigmoid(b)
    h = causal_dwconv(h, dw_w) + dw_b.astype(h.dtype)
    h = layer_norm(h, ln_w, ln_b)
    h = jax.nn.silu(h)
    return h @ pw_w.astype(h.dtype)


def chunk_gated_delta_rule(q, k, v, g, beta):
    B, S, H, DK = q.shape
    DV = v.shape[-1]
    C = GDN_CHUNK
    N = S // C
    f32 = jnp.float32

    def chunks(t):
        return t.astype(f32).reshape(B, N, C, H, -1).transpose(0, 3, 1, 2, 4)

    def chunks_h(t):
        return t.astype(f32).reshape(B, N, C, H).transpose(0, 3, 1, 2)

    q, k, v = chunks(q), chunks(k), chunks(v)
    beta = chunks_h(beta)
    g = jnp.cumsum(chunks_h(g), axis=-1)

    causal = jnp.tril(jnp.ones((C, C), dtype=bool))
    strict = jnp.tril(jnp.ones((C, C), dtype=bool), -1)
    diff = g[..., :, None] - g[..., None, :]
    decay = jnp.where(causal, jnp.exp(jnp.where(causal, diff, 0.0)), 0.0)

    kk = jnp.einsum("bhncd,bhnmd->bhncm", k, k)
    lower = jnp.where(strict, beta[..., :, None] * kk * decay, 0.0)
    eye = jnp.eye(C, dtype=f32)
    T = lax.linalg.triangular_solve(eye + lower, jnp.broadcast_to(eye, lower.shape),
                                    left_side=True, lower=True, unit_diagonal=True)
    w_v = jnp.einsum("bhncm,bhnmd->bhncd", T, v * beta[..., None])
    w_k = jnp.einsum("bhncm,bhnmd->bhncd", T, k * (beta * jnp.exp(g))[..., None])
    qk = jnp.where(causal, jnp.einsum("bhncd,bhnmd->bhncm", q, k) * decay, 0.0)
    q_dec = q * jnp.exp(g)[..., None]
    k_dec = k * jnp.exp(g[..., -1:] - g)[..., None]
    g_last = jnp.exp(g[..., -1])

    def step(state, xs):
        qk_i, qd_i, wv_i, wk_i, kd_i, gl_i = xs
        v_new = wv_i - jnp.einsum("bhcd,bhde->bhce", wk_i, state)
        o_i = jnp.einsum("bhcd,bhde->bhce", qd_i, state) + jnp.einsum("bhcm,bhme->bhce", qk_i, v_new)
        state = state * gl_i[..., None, None] + jnp.einsum("bhcd,bhce->bhde", kd_i, v_new)
        return state, o_i

    xs = (jnp.moveaxis(qk, 2, 0), jnp.moveaxis(q_dec, 2, 0), jnp.moveaxis(w_v, 2, 0),
          jnp.moveaxis(w_k, 2, 0), jnp.moveaxis(k_dec, 2, 0), jnp.moveaxis(g_last, 2, 0))
    state0 = jnp.zeros((B, H, DK, DV), f32)
    _, o = lax.scan(step, state0, xs)
    return o.transpose(1, 0, 3, 2, 4).reshape(B, S, H, DV)


def gated_deltanet(q, k, v, z, beta_in, alpha_in, conv_w, a_log, dt_bias, norm_w):
    B, S, _ = q.shape
    qkv = jax.nn.silu(causal_dwconv(jnp.concatenate([q, k, v], axis=-1), conv_w))
    q, k, v = jnp.split(qkv, [GDN_QK, 2 * GDN_QK], axis=-1)
    q = l2_normalize(q.reshape(B, S, GDN_HEADS, GDN_DK)) * (GDN_DK ** -0.5)
    k = l2_normalize(k.reshape(B, S, GDN_HEADS, GDN_DK))
    v = v.reshape(B, S, GDN_HEADS, GDN_DV)
    beta = jax.nn.sigmoid(beta_in.astype(jnp.float32))
    g = -jnp.exp(a_log.astype(jnp.float32)) * jax.nn.softplus(
        alpha_in.astype(jnp.float32) + dt_bias.astype(jnp.float32))
    o = chunk_gated_delta_rule(q, k, v, g, beta).astype(z.dtype)
    o = rms_norm(o, norm_w) * jax.nn.silu(z.reshape(B, S, GDN_HEADS, GDN_DV))
    return o.reshape(B, S, GDN_W)


def strided_window_attention(q, k, v, span, dil):
    B, S, H, E = q.shape
    unit = dil * ATT_BLOCK
    Sp = -(-S // unit) * unit
    Lr = Sp // dil
    nb = Lr // ATT_BLOCK

    def to_strided(t):
        t = jnp.pad(t, ((0, 0), (0, Sp - S), (0, 0), (0, 0)))
        return t.reshape(B, Lr, dil, H, E).transpose(0, 2, 1, 3, 4).reshape(B, dil, nb, ATT_BLOCK, H, E)

    def with_prev(t):
        prev = jnp.pad(t, ((0, 0), (0, 0), (1, 0), (0, 0), (0, 0), (0, 0)))[:, :, :-1]
        return jnp.concatenate([prev, t], axis=3)

    qs = to_strided(q)
    kb, vb = with_prev(to_strided(k)), with_prev(to_strided(v))
    s = jnp.einsum("bdnqhe,bdnkhe->bdnhqk", qs, kb) * (E ** -0.5)
    qi = jnp.arange(ATT_BLOCK)[:, None]
    ki = jnp.arange(2 * ATT_BLOCK)[None, :]
    dist = qi + ATT_BLOCK - ki
    blk = jnp.arange(nb)[:, None, None]
    valid = (dist >= 0) & (dist <= span) & ((blk - 1) * ATT_BLOCK + ki >= 0)
    s = jnp.where(valid[:, None], s, NEG_INF)
    m = jnp.max(s, axis=-1, keepdims=True)
    p = jnp.exp(s - m)
    l = jnp.sum(p, axis=-1, keepdims=True)
    o = jnp.einsum("bdnhqk,bdnkhe->bdnqhe", p / l, vb)
    lse = (m + jnp.log(l))[..., 0]
    o = o.reshape(B, dil, Lr, H, E).transpose(0, 2, 1, 3, 4).reshape(B, Sp, H, E)[:, :S]
    lse = lse.transpose(0, 1, 2, 4, 3).reshape(B, dil, Lr, H).transpose(0, 2, 1, 3).reshape(B, Sp, H)[:, :S]
    return o, lse


def dilated_attention(q, k, v, cos, sin):
    B, S, _ = q.shape
    dt = q.dtype
    q = apply_partial_rope(q.astype(jnp.float32).reshape(B, S, ATT_HEADS, ATT_HD), cos, sin)
    k = apply_partial_rope(k.astype(jnp.float32).reshape(B, S, ATT_HEADS, ATT_HD), cos, sin)
    v = v.astype(jnp.float32).reshape(B, S, ATT_HEADS, ATT_HD)
    outs, lses = [], []
    for window, dil in DIL_PATTERNS:
        o_g, lse_g = strided_window_attention(q, k, v, window // dil, dil)
        outs.append(o_g)
        lses.append(lse_g)
    wts = jax.nn.softmax(jnp.stack(lses, axis=0), axis=0)
    o = jnp.einsum("pbsh,pbshe->bshe", wts, jnp.stack(outs, axis=0))
    return o.reshape(B, S, ATT_W).astype(dt)


def setup_inputs(seed: int = 0) -> dict:
    key = jax.random.key(seed)
    ks = jax.random.split(key, 16)
    f32 = jnp.float32

    def nrm(k, shape, scale):
        return jax.random.normal(k, shape, f32) * scale

    x = nrm(ks[0], (BATCH, SEQ, D_MODEL), 1.0)
    norm_w = 1.0 + nrm(ks[1], (DEPTH, D_MODEL), 0.01)
    w_in = nrm(ks[2], (DEPTH, D_MODEL, IN_W), D_MODEL ** -0.5)
    conv_qkv_w = nrm(ks[3], (DEPTH, SHORT_CONV, 2 * GDN_QK + GDN_W), SHORT_CONV ** -0.5)
    a_log = jnp.log(jax.random.uniform(ks[4], (DEPTH, GDN_HEADS), f32, 1.0, 16.0))
    dt = jnp.exp(jax.random.uniform(ks[5], (DEPTH, GDN_HEADS), f32, math.log(1e-3), math.log(1e-1)))
    dt_bias = dt + jnp.log(-jnp.expm1(-dt))
    gdn_norm_w = 1.0 + nrm(ks[6], (DEPTH, GDN_DV), 0.01)
    conf_dw_w = nrm(ks[7], (DEPTH, CONV_WIDTH, CONV_CH), CONV_WIDTH ** -0.5)
    conf_dw_b = nrm(ks[8], (DEPTH, CONV_CH), 0.01)
    conf_ln_w = 1.0 + nrm(ks[9], (DEPTH, CONV_CH), 0.01)
    conf_ln_b = nrm(ks[10], (DEPTH, CONV_CH), 0.01)
    conf_pw_w = nrm(ks[11], (DEPTH, CONV_CH, CONV_CH), CONV_CH ** -0.5)
    w_out = nrm(ks[12], (DEPTH, MIX_W, D_MODEL), MIX_W ** -0.5)
    final_norm_w = 1.0 + nrm(ks[13], (D_MODEL,), 0.01)
    return {"x": x, "norm_w": norm_w, "w_in": w_in, "conv_qkv_w": conv_qkv_w, "a_log": a_log,
            "dt_bias": dt_bias, "gdn_norm_w": gdn_norm_w, "conf_dw_w": conf_dw_w,
            "conf_dw_b": conf_dw_b, "conf_ln_w": conf_ln_w, "conf_ln_b": conf_ln_b,
            "conf_pw_w": conf_pw_w, "w_out": w_out, "final_norm_w": final_norm_w}


def reference(x, norm_w, w_in, conv_qkv_w, a_log, dt_bias, gdn_norm_w, conf_dw_w, conf_dw_b,
              conf_ln_w, conf_ln_b, conf_pw_w, w_out, final_norm_w):
    B, S, _ = x.shape
    cos, sin = rope_tables(S)
    cuts, acc = [], 0
    for n in IN_SPLITS[:-1]:
        acc += n
        cuts.append(acc)
    for l in range(DEPTH):
        h = rms_norm(x, norm_w[l])
        u = h @ w_in[l].astype(h.dtype)
        (c_in, c_gate, g_q, g_k, g_v, g_z, g_b, g_a,
         a_q, a_k, a_v, a_gate) = jnp.split(u, cuts, axis=-1)
        y_conv = conformer_conv(c_in, conf_dw_w[l], conf_dw_b[l], conf_ln_w[l], conf_ln_b[l],
                                conf_pw_w[l]) * jax.nn.silu(c_gate)
        y_gdn = gated_deltanet(g_q, g_k, g_v, g_z, g_b, g_a, conv_qkv_w[l], a_log[l], dt_bias[l],
                               gdn_norm_w[l])
        y_att = dilated_attention(a_q, a_k, a_v, cos, sin) * jax.nn.silu(a_gate)
        y = jnp.concatenate([y_conv, y_gdn, y_att], axis=-1)
        x = x + y @ w_out[l].astype(y.dtype)
    return rms_norm(x, final_norm_w)
```

```python
import os
import numpy as np
import concourse.bass as bass
import concourse.mybir as mybir
from concourse.bass_utils import run_bass_kernel_spmd

F32 = mybir.dt.float32
BF16 = mybir.dt.bfloat16
AF = mybir.ActivationFunctionType
ALU = mybir.AluOpType
AX = mybir.AxisListType

D = 2048
NKC = 16
IN_W = 7692
PAD = 32
NEG = -1.0e30
ENGS = ["pe", "act", "dve", "pool", "sp"]


class Sched:
    def __init__(self, nc):
        self.nc = nc
        self.items = {e: [] for e in ENGS}
        self.cnt = {e: 0 for e in ENGS}
        self.sem = {}
        self.waited = {e: {} for e in ENGS}
        self.last_w = {}
        self.readers = {}
        self.dma_sems = {}
        self.dma_cnt = {}
        self.shared = set(["wc", "zp", "cw", "gw", "wo", "fw", "wrow"])
        self._ctx = []
        for e in ENGS:
            cm = nc.semaphore("prog_" + e)
            self.sem[e] = cm.__enter__()
            self._ctx.append(cm)

    def dsem(self, name):
        if name not in self.dma_sems:
            cm = self.nc.semaphore("d_" + name)
            self.dma_sems[name] = cm.__enter__()
            self._ctx.append(cm)
            self.dma_cnt[name] = 0
        return name

    def _deps(self, eng, reads, writes):
        toks = []
        for k in reads:
            t = self.last_w.get(k)
            if t is not None:
                toks.append(t)
        for k in writes:
            t = self.last_w.get(k)
            if t is not None:
                toks.append(t)
            toks.extend(self.readers.get(k, ()))
        w = self.waited[eng]
        best = {}
        for (sname, sobj, val, teng) in toks:
            if teng == eng and eng == "pe":
                continue
            if sname in self.shared:
                val = self.dma_cnt[sname]
            if w.get(sname, 0) >= val:
                continue
            if best.get(sname, (None, 0))[1] < val:
                best[sname] = (sobj, val)
        waits = []
        for sname, (sobj, val) in best.items():
            w[sname] = val
            waits.append((sobj, val))
        return waits

    def _commit(self, tok, reads, writes):
        for k in writes:
            self.last_w[k] = tok
            self.readers[k] = []
        for k in reads:
            if k not in writes:
                self.readers.setdefault(k, []).append(tok)

    def op(self, eng, fn, reads=(), writes=()):
        writes = list(writes) + [k for k in reads if k.startswith("psB") or k.startswith("psH")]
        waits = self._deps(eng, reads, writes)
        self.cnt[eng] += 1
        tok = ("prog_" + eng, self.sem[eng], self.cnt[eng], eng)
        self.items[eng].append((waits, fn, self.sem[eng], 1))
        self._commit(tok, reads, writes)

    def dma(self, q, dsem, fn, reads=(), writes=()):
        self.dsem(dsem)
        waits = self._deps(q, reads, writes)
        self.dma_cnt[dsem] += 16
        tok = (dsem, self.dma_sems[dsem], self.dma_cnt[dsem], "dma")
        self.items[q].append((waits, fn, self.dma_sems[dsem], 16))
        self._commit(tok, reads, writes)

    def barrier(self):
        for e in ENGS:
            waits = []
            w = self.waited[e]
            for o in ENGS:
                if o != e and self.cnt[o] > w.get("prog_" + o, 0):
                    waits.append((self.sem[o], self.cnt[o]))
                    w["prog_" + o] = self.cnt[o]
            for name, c in self.dma_cnt.items():
                if c > w.get(name, 0):
                    waits.append((self.dma_sems[name], c))
                    w[name] = c
            if waits:
                self.items[e].append((waits, None, None, 0))
        self.last_w = {}
        self.readers = {}

    def emit(self):
        nc = self.nc
        handles = {"pe": "tensor", "act": "scalar", "dve": "vector", "pool": "gpsimd", "sp": "sync"}
        with nc.Block() as block:
            for e in ENGS:
                items = self.items[e]
                if not items:
                    continue

                def body(engh, items=items):
                    for waits, fn, sem, inc in items:
                        if fn is None:
                            for (sobj, val) in waits:
                                engh.wait_ge(sobj, val)
                            continue
                        solo = inc == 16 or getattr(fn, "solo", False)
                        for (sobj, val) in (waits if solo else waits[:-1]):
                            engh.wait_ge(sobj, val)
                        ins = fn(engh)
                        if waits and not solo:
                            ins._wait_ge(waits[-1][0], waits[-1][1])
                        ins.then_inc(sem, inc)

                getattr(block, handles[e])(body)

    def close(self):
        for cm in reversed(self._ctx):
            cm.__exit__(None, None, None)


class Arena:
    def __init__(self, t, n):
        self.t, self.n, self.off, self.base = t, n, 0, 0

    def alloc(self, n):
        nr = (n + 15) // 16 * 16
        assert self.off + nr <= self.n, ("arena overflow", self.off, nr, self.n)
        ap = self.t[:, self.off:self.off + n]
        self.off += nr
        return ap

    def freeze(self):
        self.base = self.off

    def reset(self):
        self.off = self.base


class B:
    def __init__(self, nc, S_, L_):
        self.nc, self.S_, self.L_ = nc, S_, L_
        self.s = Sched(nc)

    def mm(self, out, lhsT, rhs, start, stop, r, w):
        self.s.op("pe", lambda e: e.matmul(out, lhsT=lhsT, rhs=rhs, start=start, stop=stop), r, w)

    def tr(self, out, in_, ident, r, w):
        self.s.op("pe", lambda e: e.transpose(out, in_, ident), r, w)

    def act(self, out, in_, func, r, w, bias=0.0, scale=1.0, accum=None):
        if accum is None:
            self.s.op("act", lambda e: e.activation(out=out, in_=in_, func=func, bias=bias, scale=scale), r, w)
        else:
            fn = lambda e: e.activation(out=out, in_=in_, func=func, bias=bias, scale=scale, accum_out=accum)
            fn.solo = True
            self.s.op("act", fn, r, w)

    def ts(self, eng, out, in0, s1, s2, op0, op1, r, w):
        if s2 is None:
            self.s.op(eng, lambda e: e.tensor_scalar(out=out, in0=in0, scalar1=s1, scalar2=None, op0=op0), r, w)
        else:
            self.s.op(eng, lambda e: e.tensor_scalar(out=out, in0=in0, scalar1=s1, scalar2=s2, op0=op0, op1=op1), r, w)

    def tt(self, eng, out, in0, in1, op, r, w):
        self.s.op(eng, lambda e: e.tensor_tensor(out=out, in0=in0, in1=in1, op=op), r, w)

    def stt(self, eng, out, in0, scalar, in1, op0, op1, r, w):
        self.s.op(eng, lambda e: e.scalar_tensor_tensor(out=out, in0=in0, scalar=scalar, in1=in1, op0=op0, op1=op1), r, w)

    def cp(self, eng, out, in_, r, w):
        if eng == "act":
            self.s.op("act", lambda e: e.copy(out=out, in_=in_), r, w)
        else:
            self.s.op(eng, lambda e: e.tensor_copy(out=out, in_=in_), r, w)

    def red(self, out, in_, op, r, w):
        self.s.op("dve", lambda e: e.tensor_reduce(out=out, in_=in_, axis=AX.X, op=op), r, w)

    def recip(self, out, in_, r, w):
        self.s.op("dve", lambda e: e.reciprocal(out=out, in_=in_), r, w)

    def memset(self, eng, ap, val, w):
        self.s.op(eng, lambda e: e.memset(ap, val), (), w)

    def asel(self, out, in_, pattern, cmp, fill, base, cm, r, w):
        self.s.op("pool", lambda e: e.affine_select(out=out, in_=in_, pattern=pattern, compare_op=cmp, fill=fill,
                                                    base=base, channel_multiplier=cm), r, w)

    def dma(self, q, sem, out, in_, r, w):
        self.s.dma(q, sem, lambda e: e.dma_start(out=out, in_=in_), r, w)

    def rsqrt(self, out, in_, mul, eps, r, w):
        self.ts("dve", out, in_, mul, eps, ALU.mult, ALU.add, r, w)
        self.act(out, out, AF.Sqrt, w, w)
        self.recip(out, out, w, w)


def v3(ap, a):
    return ap.rearrange("p (a b) -> p a b", a=a)


PHASES = ["init", "IN", "C", "G", "A", "AC", "O"]


def build_program(S_, L_, debug=False, stop=None):
    def run(ph):
        return stop is None or PHASES.index(ph) <= PHASES.index(stop)

    nc = bass.Bass("TRN2", target_bir_lowering=False)
    b = B(nc, S_, L_)
    s = b.s
    NT = S_ // 128

    def din(name, shape, dt=F32):
        return nc.dram_tensor(name, shape, dt, kind="ExternalInput").ap()

    def dscr(name, shape, dt=F32):
        return nc.dram_tensor(name, shape, dt, kind="ExternalOutput" if debug else "Internal").ap()

    x_in = din("x", [S_, D])
    norm_w = din("norm_w", [L_, D])
    w_in = din("w_in", [L_, D, IN_W])
    conv_qkv_wT = din("conv_qkv_wT", [L_, 128, 18, 4])
    a_log = din("a_log", [L_, 6])
    dt_bias = din("dt_bias", [L_, 6])
    gdn_norm_w = din("gdn_norm_w", [L_, 128])
    dw_wT = din("dw_wT", [L_, 128, 4, 31])
    dw_b = din("dw_b", [L_, 128, 4])
    ln_w = din("ln_w", [L_, 128, 4])
    ln_b = din("ln_b", [L_, 128, 4])
    pw_w = din("pw_w", [L_, 512, 512])
    w_out = din("w_out", [L_, D, D])
    fnorm_w = din("fnorm_w", [1, D])
    gmask_in = din("gmask", [128, 3, 128])
    cs_tab = din("cs_tab", [S_, 96])
    out = nc.dram_tensor("out", [S_, D], F32, kind="ExternalOutput").ap()

    wbf_in = dscr("wbf_in", [L_, D, IN_W], BF16)
    wbf_out = dscr("wbf_out", [L_, D, D], BF16)
    wbf_pw = dscr("wbf_pw", [L_, 512, 512], BF16)
    uT = dscr("uT", [30, 128, PAD + S_])
    gz = dscr("gz", [S_, 768])
    gba = dscr("gba", [S_, 12])
    aq = dscr("aq", [S_, 768], BF16)
    ak = dscr("ak", [S_, 768], BF16)
    av = dscr("av", [S_, 768], BF16)
    agate = dscr("agate", [S_, 768])
    pv = dscr("pv", [3, S_, 768])
    mlr = dscr("mlr", [3, S_, 24])
    y_tm = dscr("y_tm", [S_, 1536], BF16)
    yT_conv = dscr("yT_conv", [4, 128, S_], BF16)
    xres = dscr("xres", [S_, D])

    NF, NH = 25600, 45056
    sb_f = nc.sbuf_tensor("arena_f", [128, NF], F32).__enter__()
    sb_h = nc.sbuf_tensor("arena_h", [128, NH], BF16).__enter__()
    ps_f = nc.psum_tensor("ps_f", [128, 6, 512], F32).__enter__()
    ps_h = nc.psum_tensor("ps_h", [128, 2, 1024], BF16).__enter__()
    AFa, AHa = Arena(sb_f, NF), Arena(sb_h, NH)

    ones32 = AFa.alloc(128)
    tri32 = AFa.alloc(128)
    Mpos = AFa.alloc(128)
    Mneg2 = AFa.alloc(128)
    strict01 = AFa.alloc(128)
    matt = AFa.alloc(256)
    matt0 = AFa.alloc(256)
    ident32 = AFa.alloc(128)
    ident16 = AHa.alloc(128)
    gmask = v3(AFa.alloc(384), 3)
    AFa.freeze()
    AHa.freeze()
    b.memset("pool", ones32, 1.0, ["ones32"])
    b.memset("pool", tri32, 1.0, ["tri32"])
    b.asel(tri32, tri32, [[1, 128]], ALU.is_ge, 0.0, 0, -1, ["tri32"], ["tri32"])
    b.memset("pool", Mpos, 0.0, ["Mpos"])
    b.asel(Mpos, Mpos, [[-1, 128]], ALU.is_ge, 1.0e30, 0, 1, ["Mpos"], ["Mpos"])
    b.memset("pool", Mneg2, 0.0, ["Mneg2"])
    b.asel(Mneg2, Mneg2, [[1, 128]], ALU.is_ge, NEG, 0, -1, ["Mneg2"], ["Mneg2"])
    b.memset("pool", strict01, 1.0, ["strict01"])
    b.asel(strict01, strict01, [[-1, 128]], ALU.is_gt, 0.0, 0, 1, ["strict01"], ["strict01"])
    b.memset("pool", matt, 0.0, ["matt"])
    b.asel(matt, matt, [[1, 256]], ALU.is_ge, NEG, 0, -1, ["matt"], ["matt"])
    b.asel(matt, matt, [[-1, 256]], ALU.is_ge, NEG, 128, 1, ["matt"], ["matt"])
    b.cp("pool", matt0, matt, ["matt"], ["matt0"])
    b.memset("pool", matt0[:, 0:128], NEG, ["matt0"])
    b.memset("pool", ident32, 0.0, ["ident32"])
    b.asel(ident32, ident32, [[-1, 128]], ALU.not_equal, 1.0, 0, 1, ["ident32"], ["ident32"])
    b.cp("pool", ident16, ident32, ["ident32"], ["ident16"])
    b.dma("sp", "gmk", gmask, gmask_in, [], ["gmask"])

    for l in range(L_):
        for kc in range(NKC):
            b.dma("pool", "wc", wbf_in[l, kc * 128:(kc + 1) * 128, :], w_in[l, kc * 128:(kc + 1) * 128, :], [], ["wbf"])
            b.dma("pool", "wc", wbf_out[l, kc * 128:(kc + 1) * 128, :], w_out[l, kc * 128:(kc + 1) * 128, :], [], ["wbf"])
        b.dma("pool", "wc", wbf_pw[l], pw_w[l], [], ["wbf"])
    zt = AFa.alloc(PAD)
    b.memset("pool", zt, 0.0, ["zt"])
    for c in range(30):
        b.dma("sp", "zp", uT[c, :, 0:PAD], zt, ["zt"], ["uTpad"])
    s.barrier()

    psf_i = [0]

    nfb = [6]

    def psf():
        i = psf_i[0] % nfb[0]
        psf_i[0] += 1
        return ps_f[:, i, :], "psB%d" % i

    ps16_i = [0]

    def ps16(n):
        i = ps16_i[0] % 4
        ps16_i[0] += 1
        return ps_h[:, i // 2, (i % 2) * 512:(i % 2) * 512 + n], "psH%d" % (i // 2)

    pq_i = [0]

    def psq():
        i = pq_i[0] % ((6 - nfb[0]) * 4) + nfb[0] * 4
        pq_i[0] += 1
        return ps_f[:, i // 4, (i % 4) * 128:(i % 4) * 128 + 128], "psB%d" % (i // 4)

    for l in range(L_):
        AFa.reset()
        AHa.reset()
        xsrc = x_in if l == 0 else xres
        nfb[0] = 6
        if run("IN"):
            TG = 512
            NSUB = TG // 128
            wrow = AFa.alloc(D)
            b.dma("sp", "wrow", wrow, norm_w[l:l + 1, :].partition_broadcast(128), [], ["wrow"])
            hT = [v3(AHa.alloc(NKC * TG), NKC) for _ in range(2)]
            xt = [AFa.alloc(D) for _ in range(3)]
            hb = [AHa.alloc(D) for _ in range(2)]
            junk = AHa.alloc(D)
            ssq = [AFa.alloc(16) for _ in range(3)]
            wblk = [v3(AHa.alloc(NKC * 512), NKC) for _ in range(2)]
            stg = [AFa.alloc(512) for _ in range(4)]
            stg16 = [AHa.alloc(384) for _ in range(3)]
            rsm = [AFa.alloc(96) for _ in range(3)]
            cst = [v3(AFa.alloc(NSUB * 96), NSUB) for _ in range(2)]
            wv = wbf_in[l].rearrange("(kc p) c -> p kc c", p=128)
            blocks = []
            for i in range(7):
                blocks.append(("fm", i * 512, 512))
            blocks.append(("fm", 3584, 256))
            blocks += [("gz", 3840, 384), ("gz", 4224, 384), ("gba", 4608, 12)]
            blocks += [("aq", 4620, 384), ("aq", 5004, 384), ("ak", 5388, 384), ("ak", 5772, 384)]
            blocks += [("av", 6156, 384), ("av", 6540, 384), ("ag", 6924, 384), ("ag", 7308, 384)]
            xi = 0
            wi = 0
            si = 0
            s16i = 0
            ri = 0
            for g in range(S_ // TG):
                t0 = g * TG
                hs = g % 2
                hTk = "hT%d" % hs
                b.dma("sp", "cst%d" % hs, cst[hs], cs_tab[t0:t0 + TG, :].rearrange("(n p) c -> p n c", p=128), [], ["cst%d" % hs])
                for sub in range(NSUB):
                    xs = xi % 3
                    xi += 1
                    xk = "xt%d" % xs
                    b.dma("sp", xk, xt[xs], xsrc[t0 + sub * 128:t0 + (sub + 1) * 128, :], [], [xk])
                    b.act(junk, xt[xs], AF.Square, [xk], ["junk", "ssq%d" % xs], accum=ssq[xs][:, 0:1])
                    b.rsqrt(ssq[xs][:, 1:2], ssq[xs][:, 0:1], 1.0 / D, 1e-6, ["ssq%d" % xs], ["rs%d" % xs])
                    hbs = sub % 2
                    b.stt("dve", hb[hbs], xt[xs], ssq[xs][:, 1:2], wrow, ALU.mult, ALU.mult, [xk, "rs%d" % xs, "wrow"], ["hb%d" % hbs])
                    for q4 in range(4):
                        pt, pk = ps16(512)
                        for j in range(4):
                            kc = q4 * 4 + j
                            b.tr(pt[:, j * 128:(j + 1) * 128], hb[hbs][:, kc * 128:(kc + 1) * 128], ident16, ["hb%d" % hbs, "ident16"], [pk])
                        eng = "act" if q4 % 2 == 0 else "dve"
                        b.cp(eng, hT[hs][:, q4 * 4:q4 * 4 + 4, sub * 128:(sub + 1) * 128], v3(pt, 4), [pk], [hTk])
                for (kind, c0, ncol) in blocks:
                    ws = wi % 2
                    wi += 1
                    wk = "wblk%d" % ws
                    b.dma("sp", wk, wblk[ws][:, :, 0:ncol], wv[:, :, c0:c0 + ncol], ["wbf"], [wk])
                    if kind == "fm":
                        for j in range(ncol // 128):
                            chunk = (c0 // 128) + j
                            for half in range(TG // 512):
                                pt, pk = psf()
                                for kc in range(NKC):
                                    b.mm(pt, wblk[ws][:, kc, j * 128:(j + 1) * 128], hT[hs][:, kc, half * 512:(half + 1) * 512],
                                         kc == 0, kc == NKC - 1, [wk, hTk], [pk])
                                ss = si % 4
                                si += 1
                                b.cp("act" if ss % 2 == 0 else "dve", stg[ss], pt, [pk], ["stg%d" % ss])
                                b.dma("pool", "stg%d" % ss, uT[chunk, :, PAD + t0 + half * 512:PAD + t0 + (half + 1) * 512], stg[ss],
                                      ["stg%d" % ss], ["uT"])
                    else:
                        for sub in range(NSUB):
                            pt, pk = psf()
                            r0 = t0 + sub * 128
                            for kc in range(NKC):
                                b.mm(pt[:, 0:ncol], hT[hs][:, kc, sub * 128:(sub + 1) * 128], wblk[ws][:, kc, 0:ncol],
                                     kc == 0, kc == NKC - 1, [wk, hTk], [pk])
                            if kind in ("gz", "ag", "gba"):
                                ss = si % 4
                                si += 1
                                b.cp("act" if ss % 2 == 0 else "dve", stg[ss][:, 0:ncol], pt[:, 0:ncol], [pk], ["stg%d" % ss])
                                if kind == "gz":
                                    dst = gz[r0:r0 + 128, c0 - 3840:c0 - 3840 + ncol]
                                elif kind == "ag":
                                    dst = agate[r0:r0 + 128, c0 - 6924:c0 - 6924 + ncol]
                                else:
                                    dst = gba[r0:r0 + 128, :]
                                b.dma("pool", "stg%d" % ss, dst, stg[ss][:, 0:ncol], ["stg%d" % ss], [kind])
                            elif kind == "av":
                                ss = s16i % 3
                                s16i += 1
                                b.cp("act", stg16[ss], pt[:, 0:384], [pk], ["stg16_%d" % ss])
                                b.dma("pool", "stg16_%d" % ss, av[r0:r0 + 128, c0 - 6156:c0 - 6156 + 384], stg16[ss], ["stg16_%d" % ss], ["av"])
                            else:
                                ss = s16i % 3
                                s16i += 1
                                rs_ = ri % 3
                                ri += 1
                                sk, rk = "stg16_%d" % ss, "rsm%d" % rs_
                                p3 = v3(pt[:, 0:384], 6)
                                o3 = v3(stg16[ss], 6)
                                b.cp("act", stg16[ss], pt[:, 0:384], [pk], [sk])
                                r3 = v3(rsm[rs_], 6)
                                b.cp("act", r3, p3[:, :, 0:16], [pk], [rk])
                                cos = v3(cst[hs][:, sub, 0:48], 6)
                                sin = v3(cst[hs][:, sub, 48:96], 6)
                                tmp = v3(stg[si % 4][:, 0:192], 4)
                                tk = "stg%d" % (si % 4)
                                si += 1
                                ta, tb_, tc_, td = [v3(tmp[:, i, :], 6) for i in range(4)]
                                ck = "cst%d" % hs
                                b.tt("pool", ta, r3[:, :, 0:8], cos, ALU.mult, [rk, ck], [tk])
                                b.tt("pool", tb_, r3[:, :, 8:16], sin, ALU.mult, [rk, ck], [tk])
                                b.tt("pool", tc_, r3[:, :, 8:16], cos, ALU.mult, [rk, ck], [tk])
                                b.tt("pool", td, r3[:, :, 0:8], sin, ALU.mult, [rk, ck], [tk])
                                b.tt("pool", o3[:, :, 0:8], ta, tb_, ALU.subtract, [tk, sk], [sk])
                                b.tt("pool", o3[:, :, 8:16], tc_, td, ALU.add, [tk, sk], [sk])
                                dstt = aq if kind == "aq" else ak
                                cb = c0 - (4620 if kind == "aq" else 5388)
                                b.dma("pool", sk, dstt[r0:r0 + 128, cb:cb + 384], stg16[ss], [sk], [kind])
            s.barrier()

        if run("C"):
            AFa.reset()
            AHa.reset()
            TC = 1024 if S_ >= 1024 else S_
            HALO = 30
            dwt = v3(AFa.alloc(4 * 31), 4)
            dwbt = AFa.alloc(4)
            lnwt = AFa.alloc(4)
            lnbt = AFa.alloc(4)
            pw16 = v3(AHa.alloc(4 * 512), 4)
            b.dma("sp", "cw", dwt, dw_wT[l], [], ["dwt"])
            b.dma("sp", "cw", dwbt, dw_b[l], [], ["dwbt"])
            b.dma("sp", "cw", lnwt, ln_w[l], [], ["lnwt"])
            b.dma("sp", "cw", lnbt, ln_b[l], [], ["lnbt"])
            b.dma("sp", "cw", pw16, wbf_pw[l].rearrange("(c p) o -> p c o", p=128), ["wbf"], ["pw16"])
            at_ = [AFa.alloc(TC + HALO) for _ in range(2)]
            bt_ = [AFa.alloc(TC + HALO) for _ in range(2)]
            cacc = [AFa.alloc(TC) for _ in range(4)]
            sqt = [AFa.alloc(512) for _ in range(2)]
            meant, msq, rstdt = AFa.alloc(512), AFa.alloc(512), AFa.alloc(512)
            tmpc = [AFa.alloc(512) for _ in range(2)]
            hc = [AHa.alloc(512) for _ in range(4)]
            gt_ = [AFa.alloc(512) for _ in range(2)]
            yst = [AHa.alloc(512) for _ in range(2)]
            li = 0
            gi = 0
            for tci in range(S_ // TC):
                t0 = tci * TC
                for cc in range(4):
                    sl = li % 2
                    li += 1
                    ak_, bk_ = "cat%d" % sl, "cbt%d" % sl
                    lo = PAD + t0 - HALO
                    b.dma("sp", ak_, at_[sl], uT[cc, :, lo:lo + TC + HALO], ["uT", "uTpad"], [ak_])
                    b.dma("sp", bk_, bt_[sl], uT[4 + cc, :, lo:lo + TC + HALO], ["uT", "uTpad"], [bk_])
                    b.act(bt_[sl], bt_[sl], AF.Sigmoid, [bk_], [bk_])
                    b.tt("pool", at_[sl], at_[sl], bt_[sl], ALU.mult, [ak_, bk_], [ak_])
                    ck = "cacc%d" % cc
                    b.ts("dve", cacc[cc], at_[sl][:, 0:TC], dwt[:, cc, 0:1], dwbt[:, cc:cc + 1], ALU.mult, ALU.add,
                         [ak_, "dwt", "dwbt"], [ck])
                    for j in range(1, 31):
                        b.stt("dve", cacc[cc], at_[sl][:, j:j + TC], dwt[:, cc, j:j + 1], cacc[cc], ALU.mult, ALU.add,
                              [ak_, "dwt", ck], [ck])
                for half in range(TC // 512):
                    hsl = slice(half * 512, (half + 1) * 512)
                    pm, pmk = psf()
                    pq, pqk = psf()
                    for cc in range(4):
                        b.mm(pm, ones32, cacc[cc][:, hsl], cc == 0, cc == 3, ["ones32", "cacc%d" % cc], [pmk])
                    for cc in range(4):
                        sq = cc % 2
                        b.act(sqt[sq], cacc[cc][:, hsl], AF.Square, ["cacc%d" % cc], ["sqt%d" % sq])
                        b.mm(pq, ones32, sqt[sq], cc == 0, cc == 3, ["ones32", "sqt%d" % sq], [pqk])
                    b.act(meant, pm, AF.Copy, [pmk], ["meant"], scale=1.0 / 512)
                    b.tt("pool", msq, meant, meant, ALU.mult, ["meant"], ["msq"])
                    b.stt("dve", rstdt, pq, 1.0 / 512, msq, ALU.mult, ALU.subtract, [pqk, "msq"], ["rstdt"])
                    b.rsqrt(rstdt, rstdt, 1.0, 1e-5, ["rstdt"], ["rstdt"])
                    for cc in range(4):
                        tsl = cc % 2
                        tk = "tmpc%d" % tsl
                        b.tt("dve", tmpc[tsl], cacc[cc][:, hsl], meant, ALU.subtract, ["cacc%d" % cc, "meant"], [tk])
                        b.tt("pool", tmpc[tsl], tmpc[tsl], rstdt, ALU.mult, [tk, "rstdt"], [tk])
                        b.act(hc[cc], tmpc[tsl], AF.Silu, [tk, "lnwt", "lnbt"], ["hc%d" % cc], bias=lnbt[:, cc:cc + 1], scale=lnwt[:, cc:cc + 1])
                    for co in range(4):
                        po, pok = psf()
                        for ci in range(4):
                            b.mm(po, pw16[:, ci, co * 128:(co + 1) * 128], hc[ci], ci == 0, ci == 3, ["pw16", "hc%d" % ci], [pok])
                        gs = gi % 2
                        gi += 1
                        gk = "cg%d" % gs
                        tt0 = t0 + half * 512
                        b.dma("sp", gk, gt_[gs], uT[8 + co, :, PAD + tt0:PAD + tt0 + 512], ["uT"], [gk])
                        b.act(gt_[gs], gt_[gs], AF.Silu, [gk], [gk])
                        yk = "cy%d" % gs
                        b.tt("dve", yst[gs], po, gt_[gs], ALU.mult, [pok, gk], [yk])
                        b.dma("pool", yk, yT_conv[co, :, tt0:tt0 + 512], yst[gs], [yk], ["yTc"])
            s.barrier()

        if run("G"):
            AFa.reset()
            AHa.reset()
            SP = 128
            nfb[0] = 2
            cwt = v3(AFa.alloc(18 * 4), 18)
            b.dma("sp", "gw", cwt, conv_qkv_wT[l], [], ["cwt"])
            negA = AFa.alloc(6)
            dtb = AFa.alloc(6)
            gnw = AFa.alloc(128)
            b.dma("sp", "gw", negA, a_log[l:l + 1, :].partition_broadcast(128), [], ["negA"])
            b.dma("sp", "gw", dtb, dt_bias[l:l + 1, :].partition_broadcast(128), [], ["dtb"])
            b.dma("sp", "gw", gnw, gdn_norm_w[l:l + 1, :].partition_broadcast(128), [], ["gnw"])
            b.act(negA, negA, AF.Exp, ["negA"], ["negA"])
            b.ts("dve", negA, negA, -1.0, None, ALU.mult, None, ["negA"], ["negA"])
            raw = [v3(AFa.alloc(18 * (SP + 3)), 18) for _ in range(2)]
            cq = v3(AFa.alloc(12 * SP), 12)
            qkn = v3(AHa.alloc(12 * SP), 12)
            vT = v3(AHa.alloc(6 * SP), 6)
            sq2 = [AFa.alloc(SP) for _ in range(2)]
            rn2 = [AFa.alloc(SP) for _ in range(2)]
            zt_ = [v3(AFa.alloc(768), 6) for _ in range(2)]
            bat = [AFa.alloc(12) for _ in range(2)]
            gsm = [AFa.alloc(96) for _ in range(2)]
            state = v3(AFa.alloc(6 * 128), 6)
            state16 = v3(AHa.alloc(6 * 128), 6)
            b.memset("pool", state, 0.0, ["state%d" % h for h in range(6)])
            b.memset("pool", state16, 0.0, ["st16_%d" % h for h in range(6)])
            HF = [[AFa.alloc(128) for _ in range(6)] for _ in range(6)]
            HF2 = [[AFa.alloc(128) for _ in range(3)] for _ in range(6)]
            HS = [AFa.alloc(16) for _ in range(6)]
            WINV = [[AFa.alloc(128) for _ in range(16)] for _ in range(2)]
            HH = [[AHa.alloc(128) for _ in range(16)] for _ in range(6)]
            ysg = [v3(AHa.alloc(768), 6) for _ in range(2)]
            zsl = [v3(AFa.alloc(768), 6) for _ in range(2)]
            for sp_i in range(S_ // SP):
                ts0 = sp_i * SP
                rs = sp_i % 2
                rk = "raw%d" % rs
                b.dma("sp", rk, raw[rs], uT[12:30, :, PAD + ts0 - 3:PAD + ts0 + SP].rearrange("c p t -> p c t"), ["uT", "uTpad"], [rk])
                for c in range(18):
                    if c < 12:
                        dst, dk_ = cq[:, c, :], "cq%d" % c
                    else:
                        dst, dk_ = sq2[c % 2], "sq2_%d" % (c % 2)
                    eng = "dve"
                    b.ts(eng, dst, raw[rs][:, c, 0:SP], cwt[:, c, 0:1], None, ALU.mult, None, [rk, "cwt"], [dk_])
                    for j in range(1, 4):
                        b.stt(eng, dst, raw[rs][:, c, j:j + SP], cwt[:, c, j:j + 1], dst, ALU.mult, ALU.add, [rk, "cwt", dk_], [dk_])
                    if c < 12:
                        b.act(dst, dst, AF.Silu, [dk_], [dk_])
                    else:
                        b.act(vT[:, c - 12, :], dst, AF.Silu, [dk_], ["vT%d" % (c - 12)])
                for c in range(12):
                    sl = c % 2
                    b.act(sq2[sl], cq[:, c, :], AF.Square, ["cq%d" % c], ["sq2_%d" % sl])
                    pt, pk = psf()
                    b.mm(pt[:, 0:SP], ones32, sq2[sl], True, True, ["ones32", "sq2_%d" % sl], [pk])
                    b.rsqrt(rn2[sl], pt[:, 0:SP], 1.0, 1e-6, [pk], ["rn2_%d" % sl])
                    if c < 6:
                        b.stt("dve", qkn[:, c, :], cq[:, c, :], 128.0 ** -0.5, rn2[sl], ALU.mult, ALU.mult, ["cq%d" % c, "rn2_%d" % sl], ["qkn%d" % c])
                    else:
                        b.tt("dve", qkn[:, c, :], cq[:, c, :], rn2[sl], ALU.mult, ["cq%d" % c, "rn2_%d" % sl], ["qkn%d" % c])
                        b.cp("pool", cq[:, c, :], qkn[:, c, :], ["qkn%d" % c], ["cq%d" % c])
                for ch in range(SP // 128):
                    tch = ts0 + ch * 128
                    csl = slice(ch * 128, (ch + 1) * 128)
                    cs_ = (sp_i * (SP // 128) + ch) % 2
                    zk, bk2, gk2 = "gzt%d" % cs_, "bat%d" % cs_, "gsm%d" % cs_
                    b.dma("sp", zk, zt_[cs_], gz[tch:tch + 128, :].rearrange("p (h d) -> p h d", h=6), ["gz"], [zk])
                    b.dma("sp", bk2, bat[cs_], gba[tch:tch + 128, :], ["gba"], [bk2])
                    G = gsm[cs_]
                    beta, gg, gcA, egcA, glB, kds, glast, bsc, nbeta = (G[:, 0:6], G[:, 6:12], G[:, 12:18], G[:, 18:24], G[:, 24:30],
                                                                       G[:, 30:36], G[:, 36:42], G[:, 42:48], G[:, 48:54])
                    b.act(beta, bat[cs_][:, 0:6], AF.Sigmoid, [bk2], [gk2])
                    xg, yy, zz, z2, pp = G[:, 54:60], G[:, 60:66], G[:, 66:72], G[:, 72:78], G[:, 78:84]
                    b.tt("dve", xg, bat[cs_][:, 6:12], dtb, ALU.add, [bk2, "dtb", gk2], [gk2])
                    b.ts("dve", yy, xg, -1.0, None, ALU.mult, None, [gk2], [gk2])
                    b.tt("dve", yy, yy, xg, ALU.max, [gk2], [gk2])
                    b.act(yy, yy, AF.Exp, [gk2], [gk2], scale=-1.0)
                    b.ts("dve", zz, yy, 2.0, None, ALU.add, None, [gk2], [gk2])
                    b.recip(zz, zz, [gk2], [gk2])
                    b.tt("dve", zz, zz, yy, ALU.mult, [gk2], [gk2])
                    b.tt("dve", z2, zz, zz, ALU.mult, [gk2], [gk2])
                    b.ts("dve", pp, z2, 1.0 / 11, 1.0 / 9, ALU.mult, ALU.add, [gk2], [gk2])
                    for cf in (1.0 / 7, 1.0 / 5, 1.0 / 3, 1.0):
                        b.tt("dve", pp, pp, z2, ALU.mult, [gk2], [gk2])
                        b.ts("dve", pp, pp, cf, None, ALU.add, None, [gk2], [gk2])
                    b.stt("dve", pp, zz, 2.0, pp, ALU.mult, ALU.mult, [gk2], [gk2])
                    b.ts("dve", xg, xg, 0.0, None, ALU.max, None, [gk2], [gk2])
                    b.tt("dve", gg, pp, xg, ALU.add, [gk2], [gk2])
                    b.tt("dve", gg, gg, negA, ALU.mult, [gk2, "negA"], [gk2])
                    pt, pk = psq()
                    b.mm(pt[:, 0:6], tri32, gg, True, True, ["tri32", gk2], [pk])
                    b.cp("dve", gcA, pt[:, 0:6], [pk, gk2], [gk2])
                    pt2, pk2 = psq()
                    b.mm(pt2[:, 0:6], ones32, gg, True, True, ["ones32", gk2], [pk2])
                    b.cp("dve", glB, pt2[:, 0:6], [pk2, gk2], [gk2])
                    b.act(egcA, gcA, AF.Exp, [gk2], [gk2])
                    b.tt("dve", kds, glB, gcA, ALU.subtract, [gk2], [gk2])
                    b.act(kds, kds, AF.Exp, [gk2], [gk2])
                    b.act(glast, glB, AF.Exp, [gk2], [gk2])
                    b.tt("dve", bsc, beta, egcA, ALU.mult, [gk2], [gk2])
                    b.ts("dve", nbeta, beta, -1.0, None, ALU.mult, None, [gk2], [gk2])
                    nbsc = G[:, 84:90]
                    b.ts("dve", nbsc, bsc, -1.0, None, ALU.mult, None, [gk2], [gk2])
                    b.act(zsl[cs_], zt_[cs_], AF.Silu, [zk], ["zsl%d" % cs_])
                    for h in range(6):
                        Gh, tdec, decay, decayS, wvt, egcB = HF[h]
                        tdec2, decayT, ont = HF2[h]
                        (kb, kdec, bv, A_, Bt, P2, B2, R_, R2, qkT, wkT, qdT, vnew, P3, B3, R3) = HH[h]
                        hk = "h%d_" % h
                        qT = qkn[:, h, csl]
                        kT = qkn[:, 6 + h, csl]
                        qk_, kk_ = "qkn%d" % h, "qkn%d" % (6 + h)
                        pk_t, pkk = ps16(128)
                        b.tr(pk_t, kT, ident16, [kk_, "ident16"], [pkk])
                        b.act(kdec, pk_t, AF.Copy, [pkk, gk2], [hk + "kdec"], scale=kds[:, h:h + 1])
                        pv_t, pvk = ps16(128)
                        b.tr(pv_t, vT[:, h, csl], ident16, ["vT%d" % h, "ident16"], [pvk])
                        b.act(wvt, pv_t, AF.Copy, [pvk, gk2], [hk + "bv"], scale=beta[:, h:h + 1])
                        b.ts("pool", Gh, ones32, gg[:, h:h + 1], None, ALU.mult, None, ["ones32", gk2], [hk + "Gh"])
                        pg, pgk = psq()
                        b.mm(pg, Gh, tri32, True, True, [hk + "Gh", "tri32"], [pgk])
                        b.stt("dve", tdec, pg, gcA[:, h:h + 1], Mpos, ALU.subtract, ALU.add, [pgk, gk2, "Mpos"], [hk + "tdec"])
                        b.act(decay, tdec, AF.Exp, [hk + "tdec"], [hk + "decay"], scale=-1.0)
                        b.tt("pool", decayS, decay, strict01, ALU.mult, [hk + "decay", "strict01"], [hk + "decayS"])
                        b.stt("dve", tdec2, pg, gcA[:, h:h + 1], Mneg2, ALU.subtract, ALU.add, [pgk, gk2, "Mneg2"], [hk + "tdec2"])
                        b.act(decayT, tdec2, AF.Exp, [hk + "tdec2"], [hk + "decayT"])
                        b.act(egcB, pg, AF.Exp, [pgk], [hk + "egcB"])
                        b.tt("pool", qdT, qT, egcB, ALU.mult, [qk_, hk + "egcB"], [hk + "qdT"])
                        pkk_, pkkk = psq()
                        b.mm(pkk_, kT, kT, True, True, [kk_], [pkkk])
                        Wt = WINV[h % 2]
                        wk_ = ["winv%d_%d" % (h % 2, i) for i in range(16)]
                        (iA, iAd, iA1, iA2, iBd, iR0, iPa, iPb, iBa, iBb, iRa, iRb, iT0, iM1, iTT1, iT1) = range(16)
                        b.stt("dve", Wt[iA], pkk_, nbeta[:, h:h + 1], decayS, ALU.mult, ALU.mult, [pkkk, gk2, hk + "decayS"], [wk_[iA]])
                        b.tt("pool", Wt[iAd], Wt[iA], gmask[:, 0, :], ALU.mult, [wk_[iA], "gmask"], [wk_[iAd]])
                        b.tt("pool", Wt[iA1], Wt[iA], gmask[:, 1, :], ALU.mult, [wk_[iA], "gmask"], [wk_[iA1]])
                        b.tt("pool", Wt[iA2], Wt[iA], gmask[:, 2, :], ALU.mult, [wk_[iA], "gmask"], [wk_[iA2]])
                        pat, patk = psq()
                        b.mm(pat, Wt[iAd], ident32, True, True, [wk_[iAd], "ident32"], [patk])
                        b.cp("act", Wt[iBd], pat, [patk], [wk_[iBd]])
                        b.tt("dve", Wt[iR0], pat, ident32, ALU.add, [patk, "ident32"], [wk_[iR0]])
                        Pc, Bc, Rc = iAd, iBd, iR0
                        altP, altB, altR = (iPa, iPb), (iBa, iBb), (iRa, iRb)
                        for it in range(4):
                            Pn, Bn, Rn = altP[it % 2], altB[it % 2], altR[it % 2]
                            pp, ppk = psq()
                            b.mm(pp, Wt[Bc], Wt[Pc], True, True, [wk_[Bc], wk_[Pc]], [ppk])
                            b.cp("act", Wt[Pn], pp, [ppk], [wk_[Pn]])
                            if it < 3:
                                pb_, pbk = psq()
                                b.mm(pb_, Wt[Pc], Wt[Bc], True, True, [wk_[Bc], wk_[Pc]], [pbk])
                                b.cp("dve", Wt[Bn], pb_, [pbk], [wk_[Bn]])
                            pr, prk = psq()
                            b.mm(pr, Wt[Pn], Wt[Rc], True, True, [wk_[Pn], wk_[Rc]], [prk])
                            b.tt("dve", Wt[Rn], pr, Wt[Rc], ALU.add, [prk, wk_[Rc]], [wk_[Rn]])
                            Pc, Rc = Pn, Rn
                            if it < 3:
                                Bc = Bn
                        TTc, Tc = Rc, iT0
                        for lvl, (iAo, iTTn, iTn) in enumerate(((iA1, iTT1, iT1), (iA2, None, None))):
                            pt_, ptk = psq()
                            b.mm(pt_, Wt[TTc], ident32, True, True, [wk_[TTc], "ident32"], [ptk])
                            b.cp("act", Wt[Tc], pt_, [ptk], [wk_[Tc]])
                            pm1, pm1k = psq()
                            b.mm(pm1, Wt[iAo], Wt[TTc], True, True, [wk_[iAo], wk_[TTc]], [pm1k])
                            b.cp("act", Wt[iM1], pm1, [pm1k], [wk_[iM1]])
                            pm2, pm2k = psq()
                            b.mm(pm2, Wt[Tc], Wt[iM1], True, True, [wk_[Tc], wk_[iM1]], [pm2k])
                            if lvl == 0:
                                b.tt("dve", Wt[iTTn], pm2, Wt[TTc], ALU.add, [pm2k, wk_[TTc]], [wk_[iTTn]])
                                TTc, Tc = iTTn, iTn
                            else:
                                b.tt("dve", R_, pm2, Wt[TTc], ALU.add, [pm2k, wk_[TTc]], [hk + "R"])
                        TT, TTk = R_, hk + "R"
                        pqk, pqkk = psq()
                        b.mm(pqk, kT, qT, True, True, [kk_, qk_], [pqkk])
                        b.tt("dve", qkT, pqk, decayT, ALU.mult, [pqkk, hk + "decayT"], [hk + "qkT"])
                        stk, st16k = "state%d" % h, "st16_%d" % h
                        p1, p1k = psq()
                        b.mm(p1, cq[:, 6 + h, csl], state[:, h, :], True, True, ["cq%d" % (6 + h), stk], [p1k])
                        b.stt("dve", kb, p1, nbsc[:, h:h + 1], wvt, ALU.mult, ALU.add, [p1k, gk2, hk + "bv"], [hk + "u"])
                        pvn, pvnk = psq()
                        b.mm(pvn, TT, kb, True, True, [TTk, hk + "u"], [pvnk])
                        b.cp("act", vnew, pvn, [pvnk], [hk + "vnew"])
                        p2, p2k = psq()
                        b.mm(p2, qdT, state16[:, h, :], True, False, [hk + "qdT", st16k], [p2k])
                        b.mm(p2, qkT, vnew, False, True, [hk + "qkT", hk + "vnew"], [p2k])
                        p3, p3k = psq()
                        b.mm(p3, kdec, vnew, True, True, [hk + "kdec", hk + "vnew"], [p3k])
                        b.stt("dve", state[:, h, :], state[:, h, :], glast[:, h:h + 1], p3, ALU.mult, ALU.add, [stk, gk2, p3k], [stk])
                        b.cp("act", state16[:, h, :], state[:, h, :], [stk], [st16k])
                        b.act(ont, p2, AF.Square, [p2k], [hk + "on", hk + "ss"], accum=HS[h][:, 0:1])
                        b.rsqrt(HS[h][:, 1:2], HS[h][:, 0:1], 1.0 / 128, 1e-6, [hk + "ss"], [hk + "rs"])
                        b.stt("dve", ont, p2, HS[h][:, 1:2], gnw, ALU.mult, ALU.mult, [p2k, hk + "rs", "gnw"], [hk + "on"])
                        b.tt("pool", ysg[cs_][:, h, :], ont, zsl[cs_][:, h, :], ALU.mult, [hk + "on", "zsl%d" % cs_], ["ysg%d" % cs_])
                    b.dma("pool", "ysg%d" % cs_, y_tm[tch:tch + 128, 0:768], ysg[cs_].rearrange("p h d -> p (h d)"), ["ysg%d" % cs_], ["y_tm"])
            s.barrier()

        if run("A"):
            AFa.reset()
            AHa.reset()
            nfb[0] = 4
            qt = [AHa.alloc(768) for _ in range(2)]
            kt_ = [AHa.alloc(768) for _ in range(2)]
            vt_ = [AHa.alloc(768) for _ in range(3)]
            qTt = [v3(AHa.alloc(768), 6) for _ in range(2)]
            kTt = [v3(AHa.alloc(768), 6) for _ in range(3)]
            zero16 = AHa.alloc(768)
            b.memset("pool", zero16, 0.0, ["zero16"])
            smx = [v3(AFa.alloc(1024), 4) for _ in range(2)]
            mxs = [AFa.alloc(8) for _ in range(2)]
            pb16 = [v3(AHa.alloc(1024), 4) for _ in range(2)]
            pT16 = [v3(AHa.alloc(1024), 8) for _ in range(2)]
            pvs = [AFa.alloc(768) for _ in range(2)]
            mls = [AFa.alloc(24) for _ in range(2)]
            bi = 0
            for p_i, dil in enumerate((1, 4, 16)):
                nb = S_ // (128 * dil)
                aqv = aq.rearrange("(n d) c -> d n c", d=dil)
                akv = ak.rearrange("(n d) c -> d n c", d=dil)
                avv = av.rearrange("(n d) c -> d n c", d=dil)
                pvv = pv[p_i].rearrange("(n d) c -> d n c", d=dil)
                mlv = mlr[p_i].rearrange("(n d) c -> d n c", d=dil)
                for r in range(dil):
                    for n in range(nb):
                        q_s, k_s, v_s = bi % 2, bi % 3, bi % 3
                        kp_s = (bi - 1) % 3
                        o_s = bi % 2
                        bi += 1
                        rows = slice(n * 128, (n + 1) * 128)
                        b.dma("sp", "aqt%d" % q_s, qt[q_s], aqv[r, rows, :], ["aq"], ["qt%d" % q_s])
                        b.dma("sp", "akt%d" % q_s, kt_[q_s], akv[r, rows, :], ["ak"], ["kt%d" % q_s])
                        b.dma("sp", "avt%d" % v_s, vt_[v_s], avv[r, rows, :], ["av"], ["vt%d" % v_s])
                        for src, dstT, sk, dk2 in ((qt[q_s], qTt[q_s], "qt%d" % q_s, "qT%d" % q_s), (kt_[q_s], kTt[k_s], "kt%d" % q_s, "kT%d" % k_s)):
                            for half in range(2):
                                pt, pk = ps16(384)
                                for j in range(3):
                                    pr_ = half * 3 + j
                                    b.tr(pt[:, j * 128:(j + 1) * 128], src[:, pr_ * 128:(pr_ + 1) * 128], ident16, [sk, "ident16"], [pk])
                                b.cp("act" if half == 0 else "dve", dstT[:, half * 3:half * 3 + 3, :], v3(pt, 3), [pk], [dk2])
                        first = (n == 0)
                        ASTG = int(os.environ.get("A_STAGE", "5"))
                        if ASTG < 2:
                            continue
                        kprev = v3(zero16, 6) if first else kTt[kp_s]
                        kpk = "zero16" if first else "kT%d" % kp_s
                        vprev = zero16 if first else vt_[kp_s]
                        vpk = "zero16" if first else "vt%d" % kp_s
                        mask = matt0 if first else matt
                        mk = "matt0" if first else "matt"
                        pvk_ = "pvs%d" % o_s
                        mlk = "mls%d" % o_s
                        for hg in range(3):
                            ws_ = hg % 2
                            p0, p0k = psf()
                            p1, p1k = psf()
                            sc = [v3(p0, 2), v3(p1, 2)]
                            sck = [p0k, p1k]
                            for hh in range(4):
                                hd = hg * 4 + hh
                                pr_, off = hd // 2, (hd % 2) * 64
                                o = sc[hh % 2][:, hh // 2, :]
                                b.mm(o[:, 0:128], qTt[q_s][off:off + 64, pr_, :], kprev[off:off + 64, pr_, :], True, True,
                                     ["qT%d" % q_s, kpk], [sck[hh % 2]])
                                b.mm(o[:, 128:256], qTt[q_s][off:off + 64, pr_, :], kTt[k_s][off:off + 64, pr_, :], True, True,
                                     ["qT%d" % q_s, "kT%d" % k_s], [sck[hh % 2]])
                            ASUB = int(os.environ.get("A_SUB", "9"))
                            if ASUB < 1:
                                continue
                            smk = "smx%d" % ws_
                            for hh in range(4):
                                b.stt("dve", smx[ws_][:, hh, :], sc[hh % 2][:, hh // 2, :], 0.125, mask, ALU.mult, ALU.add, [sck[hh % 2], mk], [smk])
                            mxk = "mxs%d" % ws_
                            if ASUB < 2:
                                continue
                            b.red(mxs[ws_][:, 0:4], smx[ws_], ALU.max, [smk], [mxk])
                            b.ts("dve", mxs[ws_][:, 4:8], mxs[ws_][:, 0:4], -1.0, None, ALU.mult, None, [mxk], [mxk])
                            b.cp("pool", mls[o_s][:, hg * 4:hg * 4 + 4], mxs[ws_][:, 0:4], [mxk], [mlk])
                            pbk = "pb16_%d" % ws_
                            if ASUB < 3:
                                continue
                            for hh in range(4):
                                b.act(pb16[ws_][:, hh, :], smx[ws_][:, hh, :], AF.Exp, [smk, mxk], [pbk, mlk], bias=mxs[ws_][:, 4 + hh:5 + hh],
                                      accum=mls[o_s][:, 12 + hg * 4 + hh:13 + hg * 4 + hh])
                            if ASTG < 3:
                                continue
                            pTk = "pT16_%d" % ws_
                            for hp in range(2):
                                pt, pk = ps16(512)
                                for j in range(4):
                                    hh, kb_ = hp * 2 + j // 2, j % 2
                                    b.tr(pt[:, j * 128:(j + 1) * 128], pb16[ws_][:, hh, kb_ * 128:(kb_ + 1) * 128], ident16, [pbk, "ident16"], [pk])
                                b.cp("act" if hp == 0 else "dve", pT16[ws_][:, hp * 4:hp * 4 + 4, :], v3(pt, 4), [pk], [pTk])
                            if ASTG < 4:
                                continue
                            po, pok = psq()
                            p_o2, pok2 = psq()
                            for hh in range(4):
                                hd = hg * 4 + hh
                                o = (po if hh < 2 else p_o2)[:, (hh % 2) * 64:(hh % 2) * 64 + 64]
                                okk = pok if hh < 2 else pok2
                                b.mm(o, pT16[ws_][:, hh * 2, :], vprev[:, hd * 64:(hd + 1) * 64], True, False, [pTk, vpk], [okk])
                                b.mm(o, pT16[ws_][:, hh * 2 + 1, :], vt_[v_s][:, hd * 64:(hd + 1) * 64], False, True, [pTk, "vt%d" % v_s], [okk])
                            b.cp("act", pvs[o_s][:, hg * 256:hg * 256 + 128], po, [pok], [pvk_])
                            b.cp("act", pvs[o_s][:, hg * 256 + 128:hg * 256 + 256], p_o2, [pok2], [pvk_])
                        if ASTG < 5:
                            continue
                        b.dma("pool", pvk_, pvv[r, rows, :], pvs[o_s], [pvk_], ["pv"])
                        b.dma("pool", mlk, mlv[r, rows, :], mls[o_s], [mlk], ["mlr"])
            s.barrier()
        if run("AC"):
            AFa.reset()
            AHa.reset()
            pvt = [[v3(AFa.alloc(768), 12) for _ in range(3)] for _ in range(2)]
            mlt = [[AFa.alloc(24) for _ in range(3)] for _ in range(2)]
            agt = [v3(AFa.alloc(768), 12) for _ in range(2)]
            acc = [v3(AFa.alloc(768), 12) for _ in range(2)]
            sm_ = [AFa.alloc(96) for _ in range(2)]
            yo = [AHa.alloc(768) for _ in range(2)]
            for t in range(NT):
                sl = t % 2
                rows = slice(t * 128, (t + 1) * 128)
                for p_i in range(3):
                    b.dma("sp", "cpv%d_%d" % (sl, p_i), pvt[sl][p_i], pv[p_i, rows, :].rearrange("p (h d) -> p h d", h=12), ["pv"], ["cpv%d_%d" % (sl, p_i)])
                    b.dma("sp", "cml%d_%d" % (sl, p_i), mlt[sl][p_i], mlr[p_i, rows, :], ["mlr"], ["cml%d_%d" % (sl, p_i)])
                b.dma("sp", "cag%d" % sl, agt[sl], agate[rows, :].rearrange("p (h d) -> p h d", h=12), ["ag"], ["cag%d" % sl])
                W = sm_[sl]
                wk_ = "csm%d" % sl
                M_, e0, den, tmp_ = W[:, 0:12], [W[:, 12 + 12 * i:24 + 12 * i] for i in range(3)], W[:, 48:60], W[:, 60:72]
                mlks = ["cml%d_%d" % (sl, i) for i in range(3)]
                b.tt("dve", M_, mlt[sl][0][:, 0:12], mlt[sl][1][:, 0:12], ALU.max, mlks, [wk_])
                b.tt("dve", M_, M_, mlt[sl][2][:, 0:12], ALU.max, mlks + [wk_], [wk_])
                for i in range(3):
                    b.tt("dve", e0[i], mlt[sl][i][:, 0:12], M_, ALU.subtract, mlks + [wk_], [wk_])
                    b.act(e0[i], e0[i], AF.Exp, [wk_], [wk_])
                b.tt("dve", den, e0[0], mlt[sl][0][:, 12:24], ALU.mult, mlks + [wk_], [wk_])
                for i in (1, 2):
                    b.tt("dve", tmp_, e0[i], mlt[sl][i][:, 12:24], ALU.mult, mlks + [wk_], [wk_])
                    b.tt("dve", den, den, tmp_, ALU.add, [wk_], [wk_])
                b.recip(den, den, [wk_], [wk_])
                for i in range(3):
                    b.tt("dve", e0[i], e0[i], den, ALU.mult, [wk_], [wk_])
                ak2 = "cacc%d" % sl
                for hd in range(12):
                    eng = "dve"
                    b.ts(eng, acc[sl][:, hd, :], pvt[sl][0][:, hd, :], e0[0][:, hd:hd + 1], None, ALU.mult, None, ["cpv%d_0" % sl, wk_], [ak2])
                    for i in (1, 2):
                        b.stt(eng, acc[sl][:, hd, :], pvt[sl][i][:, hd, :], e0[i][:, hd:hd + 1], acc[sl][:, hd, :], ALU.mult, ALU.add,
                              ["cpv%d_%d" % (sl, i), wk_, ak2], [ak2])
                b.act(agt[sl], agt[sl], AF.Silu, ["cag%d" % sl], ["cag%d" % sl])
                b.tt("dve", yo[sl], acc[sl].rearrange("p h d -> p (h d)"), agt[sl].rearrange("p h d -> p (h d)"), ALU.mult, [ak2, "cag%d" % sl], ["cyo%d" % sl])
                b.dma("pool", "cyo%d" % sl, y_tm[rows, 768:1536], yo[sl], ["cyo%d" % sl], ["y_tm"])
            s.barrier()

        if run("O"):
            AFa.reset()
            AHa.reset()
            last = (l == L_ - 1)
            nfb[0] = 6
            wo16 = v3(AHa.alloc(NKC * D), NKC)
            wov = wbf_out[l].rearrange("(kc p) c -> p kc c", p=128)
            for kc in range(NKC):
                b.dma("sp", "wo", wo16[:, kc, :], wov[:, kc, :], ["wbf"], ["wo16"])
            fw = None
            if last:
                fw = AFa.alloc(D)
                b.dma("sp", "fw", fw, fnorm_w[0:1, :].partition_broadcast(128), [], ["fw"])
            ycT = [v3(AHa.alloc(4 * 128), 4) for _ in range(2)]
            ytm = [AHa.alloc(1536) for _ in range(2)]
            yT2 = [v3(AHa.alloc(12 * 128), 12) for _ in range(2)]
            xo = [AFa.alloc(D) for _ in range(3)]
            ost = [AFa.alloc(D) for _ in range(2)]
            junk2 = AHa.alloc(D)
            ss2 = [AFa.alloc(16) for _ in range(2)]
            for t in range(NT):
                sl = t % 2
                xs = t % 3
                rows = slice(t * 128, (t + 1) * 128)
                b.dma("sp", "oyc%d" % sl, ycT[sl], yT_conv[:, :, rows].rearrange("c p t -> p c t"), ["yTc"], ["oyc%d" % sl])
                b.dma("sp", "oyt%d" % sl, ytm[sl], y_tm[rows, :], ["y_tm"], ["oyt%d" % sl])
                b.dma("sp", "ox%d" % xs, xo[xs], xsrc[rows, :], [], ["ox%d" % xs])
                for q3 in range(3):
                    pt, pk = ps16(512)
                    for j in range(4):
                        c = q3 * 4 + j
                        b.tr(pt[:, j * 128:(j + 1) * 128], ytm[sl][:, c * 128:(c + 1) * 128], ident16, ["oyt%d" % sl, "ident16"], [pk])
                    b.cp("act" if q3 % 2 == 0 else "dve", yT2[sl][:, q3 * 4:q3 * 4 + 4, :], v3(pt, 4), [pk], ["oyT%d" % sl])
                for db in range(4):
                    pt, pk = psf()
                    for kc in range(NKC):
                        if kc < 4:
                            lt, lk = ycT[sl][:, kc, :], "oyc%d" % sl
                        else:
                            lt, lk = yT2[sl][:, kc - 4, :], "oyT%d" % sl
                        b.mm(pt, lt, wo16[:, kc, db * 512:(db + 1) * 512], kc == 0, kc == NKC - 1, [lk, "wo16"], [pk])
                    b.tt("dve", xo[xs][:, db * 512:(db + 1) * 512], pt, xo[xs][:, db * 512:(db + 1) * 512], ALU.add, [pk, "ox%d" % xs], ["ox%d" % xs])
                if not last:
                    b.dma("pool", "ox%d" % xs, xres[rows, :], xo[xs], ["ox%d" % xs], ["xres"])
                else:
                    b.act(junk2, xo[xs], AF.Square, ["ox%d" % xs], ["junk2", "oss%d" % sl], accum=ss2[sl][:, 0:1])
                    b.rsqrt(ss2[sl][:, 1:2], ss2[sl][:, 0:1], 1.0 / D, 1e-6, ["oss%d" % sl], ["ors%d" % sl])
                    b.stt("dve", ost[sl], xo[xs], ss2[sl][:, 1:2], fw, ALU.mult, ALU.mult, ["ox%d" % xs, "ors%d" % sl, "fw"], ["ost%d" % sl])
                    b.dma("pool", "ost%d" % sl, out[rows, :], ost[sl], ["ost%d" % sl], ["out"])
            s.barrier()

    s.emit()
    return nc


def _rope_table(S_):
    half = 8
    inv = (np.float32(500000.0) ** (-np.arange(half, dtype=np.float32) / np.float32(half))).astype(np.float32)
    ang = (np.arange(S_, dtype=np.float32)[:, None] * inv[None, :]).astype(np.float32)
    cos, sin = np.cos(ang).astype(np.float32), np.sin(ang).astype(np.float32)
    tab = np.concatenate([np.tile(cos, (1, 6)), np.tile(sin, (1, 6))], axis=1)
    return np.ascontiguousarray(tab.astype(np.float32))


def _gdn_masks():
    i = np.arange(128)
    r, c = i[:, None], i[None, :]
    bd = (r // 32 == c // 32)
    off1 = (r // 64 == c // 64) & (r // 32 > c // 32)
    off2 = (r // 64 > c // 64)
    return np.ascontiguousarray(np.stack([bd, off1, off2], axis=1).astype(np.float32))


_PROG_CACHE = {}


def run_module(x, norm_w, w_in, conv_qkv_w, a_log, dt_bias, gdn_norm_w, conf_dw_w, conf_dw_b, conf_ln_w, conf_ln_b,
               conf_pw_w, w_out, final_norm_w, n_cores=8, prog=None, raw=False):
    Bn, S_, _ = x.shape
    L_ = w_in.shape[0]
    f = lambda a: np.ascontiguousarray(np.asarray(a, dtype=np.float32))
    key = (S_, L_)
    if prog is not None:
        nc = prog
    else:
        if key not in _PROG_CACHE:
            _PROG_CACHE[key] = build_program(S_, L_)
        nc = _PROG_CACHE[key]
    shared = {
        "norm_w": f(norm_w), "w_in": f(w_in),
        "conv_qkv_wT": f(np.transpose(np.asarray(conv_qkv_w), (0, 2, 1)).reshape(L_, 18, 128, 4).transpose(0, 2, 1, 3)),
        "a_log": f(a_log), "dt_bias": f(dt_bias), "gdn_norm_w": f(gdn_norm_w),
        "dw_wT": f(np.transpose(np.asarray(conf_dw_w), (0, 2, 1)).reshape(L_, 4, 128, 31).transpose(0, 2, 1, 3)),
        "dw_b": f(np.asarray(conf_dw_b).reshape(L_, 4, 128).transpose(0, 2, 1)), "ln_w": f(np.asarray(conf_ln_w).reshape(L_, 4, 128).transpose(0, 2, 1)),
        "ln_b": f(np.asarray(conf_ln_b).reshape(L_, 4, 128).transpose(0, 2, 1)), "pw_w": f(conf_pw_w), "w_out": f(w_out),
        "fnorm_w": f(np.asarray(final_norm_w).reshape(1, D)), "cs_tab": _rope_table(S_), "gmask": _gdn_masks(),
    }
    xs = f(x)
    in_maps = []
    for c in range(n_cores):
        m = dict(shared)
        m["x"] = xs[(c * Bn) // n_cores]
        in_maps.append(m)
    res = run_bass_kernel_spmd(nc, in_maps, core_ids=list(range(n_cores)))
    if raw:
        return res.results
    outp = np.empty((Bn, S_, D), np.float32)
    per = n_cores // Bn
    for bi in range(Bn):
        for h in range(per):
            lo, hi = (S_ * h) // per, (S_ * (h + 1)) // per
            outp[bi, lo:hi] = res.results[bi * per + h]["out"][lo:hi]
    return outp


def kernel(**inputs):
    return run_module(**inputs)
```
